# Optimizing a Trainium2 kernel written in Bass

```python
import math
import jax, jax.numpy as jnp
from jax import lax
import numpy as np

D_MODEL = 2048
BATCH = 1
SEQ = 16384
DEPTH = 2

N_MIXERS = 2
NORM_EPS = 1e-6
S5_GROUP = 16
S5_GROUPS = D_MODEL // S5_GROUP
S5_STATE = 64
S5_CHUNK = 1024
HEAD_DIM = 64
N_HEADS = D_MODEL // HEAD_DIM
KV_HEADS = N_HEADS // 8
GROUP = N_HEADS // KV_HEADS
Q_DIM = N_HEADS * HEAD_DIM
KV_DIM = KV_HEADS * HEAD_DIM
QKV_DIM = Q_DIM + 2 * KV_DIM
WINDOW = 128
ATTN_BLOCK = 128
D_FF = ((8 * D_MODEL // 3 + 255) // 256) * 256

kernel_name = "hybrid_s5_swa_sink_alibi_swiglu"


def rmsnorm(x, g):
    xf = x.astype(jnp.float32)
    r = xf * lax.rsqrt(jnp.mean(xf * xf, axis=-1, keepdims=True) + NORM_EPS)
    return (r * g.astype(jnp.float32)).astype(x.dtype)


def alibi_slopes():
    h = jnp.arange(1, N_HEADS + 1, dtype=jnp.float32)
    return jnp.exp2(-8.0 * h / N_HEADS)


def _ssm_combine(e1, e2):
    a1r, a1i, b1r, b1i = e1
    a2r, a2i, b2r, b2i = e2
    ar = a2r * a1r - a2i * a1i
    ai = a2r * a1i + a2i * a1r
    br = a2r * b1r - a2i * b1i + b2r
    bi = a2r * b1i + a2i * b1r + b2i
    return ar, ai, br, bi


def s5_mixer(u, a_re, a_im, log_step, b_re, b_im, c_re, c_im, d_skip, w_glu):
    bsz, seq, dm = u.shape
    f32 = jnp.float32
    uf = u.astype(f32)
    dt = jnp.exp(log_step.astype(f32))[:, None]
    lam_re, lam_im = a_re.astype(f32), a_im.astype(f32)
    mag = jnp.exp(lam_re * dt)
    ang = lam_im * dt
    lb_re, lb_im = mag * jnp.cos(ang), mag * jnp.sin(ang)
    n_re, n_im = lb_re - 1.0, lb_im
    den = lam_re * lam_re + lam_im * lam_im
    q_re = (n_re * lam_re + n_im * lam_im) / den
    q_im = (n_im * lam_re - n_re * lam_im) / den
    br_, bi_ = b_re.astype(f32), b_im.astype(f32)
    bb_re = q_re[..., None] * br_ - q_im[..., None] * bi_
    bb_im = q_re[..., None] * bi_ + q_im[..., None] * br_
    cr_, ci_ = c_re.astype(f32), c_im.astype(f32)

    chunk = math.gcd(seq, S5_CHUNK)
    n_chunks = seq // chunk
    u_chunks = uf.reshape(bsz, n_chunks, chunk, S5_GROUPS, S5_GROUP).transpose(1, 0, 2, 3, 4)

    def step(carry, uc):
        h_re, h_im = carry
        bu_re = jnp.einsum('blgc,gpc->blgp', uc, bb_re)
        bu_im = jnp.einsum('blgc,gpc->blgp', uc, bb_im)
        a_re_b = jnp.broadcast_to(lb_re, bu_re.shape)
        a_im_b = jnp.broadcast_to(lb_im, bu_re.shape)
        ar, ai, sr, si = lax.associative_scan(_ssm_combine, (a_re_b, a_im_b, bu_re, bu_im), axis=1)
        s_re = ar * h_re[:, None] - ai * h_im[:, None] + sr
        s_im = ar * h_im[:, None] + ai * h_re[:, None] + si
        y = jnp.einsum('blgp,gcp->blgc', s_re, cr_) - jnp.einsum('blgp,gcp->blgc', s_im, ci_)
        return (s_re[:, -1], s_im[:, -1]), y

    init = (jnp.zeros((bsz, S5_GROUPS, S5_STATE), f32), jnp.zeros((bsz, S5_GROUPS, S5_STATE), f32))
    _, ys = lax.scan(step, init, u_chunks)
    y = ys.transpose(1, 0, 2, 3, 4).reshape(bsz, seq, dm)
    y = y + d_skip.astype(f32) * uf
    g = jax.nn.gelu(y).astype(u.dtype)
    gl = g @ w_glu
    out = gl[..., :dm] * jax.nn.sigmoid(gl[..., dm:])
    return out.astype(u.dtype)


def sliding_window_attention(x, w_qkv, b_qkv, sinks, w_o):
    bsz, seq, _ = x.shape
    nb = seq // ATTN_BLOCK
    qkv = x @ w_qkv + b_qkv
    q = qkv[..., :Q_DIM].reshape(bsz, nb, ATTN_BLOCK, KV_HEADS, GROUP, HEAD_DIM)
    k = qkv[..., Q_DIM:Q_DIM + KV_DIM].reshape(bsz, seq, KV_HEADS, HEAD_DIM)
    v = qkv[..., Q_DIM + KV_DIM:].reshape(bsz, seq, KV_HEADS, HEAD_DIM)

    def band(t):
        tp = jnp.pad(t, ((0, 0), (ATTN_BLOCK, 0), (0, 0), (0, 0)))
        tb = tp.reshape(bsz, nb + 1, ATTN_BLOCK, KV_HEADS, HEAD_DIM)
        return jnp.concatenate([tb[:, :-1], tb[:, 1:]], axis=2)

    kw, vw = band(k), band(v)
    s = jnp.einsum('bnqhgd,bnkhd->bnhgqk', q, kw).astype(jnp.float32) * (HEAD_DIM ** -0.5)
    qi = jnp.arange(ATTN_BLOCK)[:, None]
    kj = jnp.arange(2 * ATTN_BLOCK)[None, :]
    dist = qi + ATTN_BLOCK - kj
    kpos = jnp.arange(nb)[:, None] * ATTN_BLOCK + jnp.arange(2 * ATTN_BLOCK)[None, :] - ATTN_BLOCK
    valid = ((dist >= 0) & (dist < WINDOW))[None] & (kpos >= 0)[:, None, :]
    slopes = alibi_slopes().reshape(KV_HEADS, GROUP)[:, :, None, None]
    s = s - slopes * dist.astype(jnp.float32)
    s = jnp.where(valid[None, :, None, None], s, -jnp.inf)
    sink = sinks.astype(jnp.float32).reshape(KV_HEADS, GROUP)[:, :, None, None]
    m = jnp.maximum(jnp.max(s, axis=-1, keepdims=True), sink)
    p = jnp.exp(s - m)
    p = p / (jnp.sum(p, axis=-1, keepdims=True) + jnp.exp(sink - m))
    o = jnp.einsum('bnhgqk,bnkhd->bnqhgd', p.astype(vw.dtype), vw).reshape(bsz, seq, Q_DIM)
    return (o @ w_o).astype(x.dtype)


def swiglu(x, w_gate, w_up, w_down):
    return (jax.nn.silu(x @ w_gate) * (x @ w_up)) @ w_down


def setup_inputs(seed: int = 0) -> dict:
    key = jax.random.key(seed)
    ks = jax.random.split(key, 20)
    n_a = (DEPTH + 1) // 2
    n_b = DEPTH // 2
    nrm = jax.random.normal
    f32 = jnp.float32
    x = nrm(ks[0], (BATCH, SEQ, D_MODEL), f32)
    norm_mix = 1.0 + 0.02 * nrm(ks[1], (DEPTH, D_MODEL), f32)
    s5_a_re = -0.5 * jnp.exp(0.05 * nrm(ks[2], (n_a, S5_GROUPS, S5_STATE), f32))
    s5_a_im = math.pi * jnp.arange(S5_STATE, dtype=f32)[None, None, :] + 0.02 * nrm(ks[3], (n_a, S5_GROUPS, S5_STATE), f32)
    s5_log_step = jax.random.uniform(ks[4], (n_a, S5_GROUPS), f32, minval=math.log(1e-3), maxval=math.log(1e-1))
    s5_b_re = nrm(ks[5], (n_a, S5_GROUPS, S5_STATE, S5_GROUP), f32) * (2 * S5_GROUP) ** -0.5
    s5_b_im = nrm(ks[6], (n_a, S5_GROUPS, S5_STATE, S5_GROUP), f32) * (2 * S5_GROUP) ** -0.5
    s5_c_re = nrm(ks[7], (n_a, S5_GROUPS, S5_GROUP, S5_STATE), f32) * S5_STATE ** -0.5
    s5_c_im = nrm(ks[8], (n_a, S5_GROUPS, S5_GROUP, S5_STATE), f32) * S5_STATE ** -0.5
    s5_d = nrm(ks[9], (n_a, D_MODEL), f32)
    s5_w_glu = nrm(ks[10], (n_a, D_MODEL, 2 * D_MODEL), f32) * D_MODEL ** -0.5
    attn_w_qkv = nrm(ks[11], (n_b, D_MODEL, QKV_DIM), f32) * D_MODEL ** -0.5
    attn_b_qkv = 0.02 * nrm(ks[12], (n_b, QKV_DIM), f32)
    attn_sinks = 0.5 * nrm(ks[13], (n_b, N_HEADS), f32)
    attn_w_o = nrm(ks[14], (n_b, Q_DIM, D_MODEL), f32) * Q_DIM ** -0.5
    norm_ffn = 1.0 + 0.02 * nrm(ks[15], (DEPTH, D_MODEL), f32)
    ffn_w_gate = nrm(ks[16], (DEPTH, D_MODEL, D_FF), f32) * D_MODEL ** -0.5
    ffn_w_up = nrm(ks[17], (DEPTH, D_MODEL, D_FF), f32) * D_MODEL ** -0.5
    ffn_w_down = nrm(ks[18], (DEPTH, D_FF, D_MODEL), f32) * D_FF ** -0.5
    norm_final = 1.0 + 0.02 * nrm(ks[19], (D_MODEL,), f32)
    return {"x": x, "norm_mix": norm_mix, "s5_a_re": s5_a_re, "s5_a_im": s5_a_im,
            "s5_log_step": s5_log_step, "s5_b_re": s5_b_re, "s5_b_im": s5_b_im,
            "s5_c_re": s5_c_re, "s5_c_im": s5_c_im, "s5_d": s5_d, "s5_w_glu": s5_w_glu,
            "attn_w_qkv": attn_w_qkv, "attn_b_qkv": attn_b_qkv, "attn_sinks": attn_sinks,
            "attn_w_o": attn_w_o, "norm_ffn": norm_ffn, "ffn_w_gate": ffn_w_gate,
            "ffn_w_up": ffn_w_up, "ffn_w_down": ffn_w_down, "norm_final": norm_final}


def reference(x, norm_mix, s5_a_re, s5_a_im, s5_log_step, s5_b_re, s5_b_im, s5_c_re, s5_c_im,
              s5_d, s5_w_glu, attn_w_qkv, attn_b_qkv, attn_sinks, attn_w_o, norm_ffn,
              ffn_w_gate, ffn_w_up, ffn_w_down, norm_final):
    h = x
    for i in range(DEPTH):
        j = i // N_MIXERS
        hn = rmsnorm(h, norm_mix[i])
        if i % N_MIXERS == 0:
            mix = s5_mixer(hn, s5_a_re[j], s5_a_im[j], s5_log_step[j], s5_b_re[j], s5_b_im[j],
                           s5_c_re[j], s5_c_im[j], s5_d[j], s5_w_glu[j])
        else:
            mix = sliding_window_attention(hn, attn_w_qkv[j], attn_b_qkv[j], attn_sinks[j], attn_w_o[j])
        h = h + mix
        hn = rmsnorm(h, norm_ffn[i])
        h = h + swiglu(hn, ffn_w_gate[i], ffn_w_up[i], ffn_w_down[i]).astype(h.dtype)
    return rmsnorm(h, norm_final)
```

```python
import os
from contextlib import ExitStack
import numpy as np
import concourse.bass as bass
import concourse.mybir as mybir
from concourse.bass_utils import run_bass_kernel_spmd

F32 = mybir.dt.float32
BF16 = mybir.dt.bfloat16
ALU = mybir.AluOpType
AF = mybir.ActivationFunctionType
AX = mybir.AxisListType

NCORES = 8
NPASS = int(os.environ.get("MK_NPASS", "2"))
NT = 1024
D = 2048
KC = 16
DFF = 5632
FCB = 11
NFB = 4
QKV = 2560
EPS = 1e-6
STAGE = os.environ.get("MK_STAGE", "E")
SKIP_S5 = os.environ.get("MK_SKIP_S5", "0") == "1"
ATT_LEVEL = int(os.environ.get("MK_ATT", "4"))
LITE = os.environ.get("MK_LITE", "0") == "1"
CORE_T = int(os.environ.get("MK_CORE_T", "8"))
CORE_C = int(os.environ.get("MK_CORE_C", "16"))
CORE_S = int(os.environ.get("MK_CORE_S", "9"))
NOCC = os.environ.get("MK_NOCC", "0") == "1"

ENGS = ["pe", "act", "dve", "pool", "sp"]


class Buf:
    __slots__ = ("name", "w", "r", "dcount", "depoch", "dsems", "dhist")

    def __init__(self, name):
        self.name = name
        self.w = None
        self.r = []
        self.dcount = 0
        self.depoch = 0
        self.dsems = None
        self.dhist = []


class Ins:
    __slots__ = ("eng", "fn", "idx", "waits", "signal", "rank", "dbuf", "inc", "depoch", "epoch")

    def __init__(self, eng, fn):
        self.eng = eng
        self.fn = fn
        self.waits = []
        self.signal = False
        self.rank = None
        self.epoch = 0
        self.dbuf = None
        self.depoch = 0
        self.inc = 16


EPOCH_E = 4000
EPOCH_D = 4000


def _tkey(t):
    return t[1] if t[0] == "E" else (id(t[1]), t[3])


class Prog:
    def __init__(self):
        self.ins = {e: [] for e in ENGS}
        self.seen = {e: {} for e in ENGS}
        self.dbufs = []

    def _need(self, eng, tok, ins):
        key, val = _tkey(tok), tok[2]
        if self.seen[eng].get(key, -1) >= val:
            return
        self.seen[eng][key] = val
        ins.waits.append(tok)
        if tok[0] == "E":
            self.ins[tok[1]][tok[2]].signal = True

    def _deps(self, eng, reads, writes, ins):
        toks = {}

        def add(t):
            key = _tkey(t)
            if key not in toks or toks[key][2] < t[2]:
                toks[key] = t
        for b in reads:
            if b.w is not None:
                add(b.w)
        for b in writes:
            if b.w is not None and not (b.w[0] == "E" and b.w[1] == eng):
                add(b.w)
            for t in b.r:
                if not (t[0] == "E" and t[1] == eng):
                    add(t)
        for t in toks.values():
            self._need(eng, t, ins)

    def _post(self, tok, reads, writes):
        for b in writes:
            b.w = tok
            b.r = []
        key = _tkey(tok)
        for b in reads:
            if b not in writes:
                b.r = [t for t in b.r if _tkey(t) != key] + [tok]

    def op(self, eng, fn, reads=(), writes=()):
        ins = Ins(eng, fn)
        ins.idx = len(self.ins[eng])
        self._deps(eng, reads, writes, ins)
        self.ins[eng].append(ins)
        tok = ("E", eng, ins.idx)
        self._post(tok, reads, writes)
        return tok

    def dma(self, eng, fn, reads=(), writes=(), sync=None, inc=16):
        ins = Ins(eng, fn)
        ins.idx = len(self.ins[eng])
        ins.inc = inc
        self._deps(eng, reads, writes, ins)
        self.ins[eng].append(ins)
        if sync is None:
            sync = writes[0] if writes else reads[0]
        if sync.dsems is None:
            sync.dsems = []
            self.dbufs.append(sync)
        if sync.dcount + inc > EPOCH_D:
            sync.dhist.append(sync.dcount)
            sync.depoch += 1
            sync.dcount = 0
        sync.dcount += inc
        ins.dbuf = sync
        ins.depoch = sync.depoch
        tok = ("D", sync, sync.dcount, sync.depoch)
        self._post(tok, reads, writes)
        return tok

    def _all_tokens(self):
        toks = []
        for e in ENGS:
            for ins in reversed(self.ins[e]):
                if ins.fn is not None and ins.dbuf is None:
                    toks.append(("E", e, ins.idx))
                    break
        for b in self.dbufs:
            for ep, c in enumerate(b.dhist):
                toks.append(("D", b, c, ep))
            toks.append(("D", b, b.dcount, b.depoch))
        return toks

    def barrier(self):
        toks = self._all_tokens()
        for e in ENGS:
            ins = Ins(e, None)
            ins.idx = len(self.ins[e])
            for t in toks:
                if t[0] == "E" and t[1] == e:
                    continue
                self._need(e, t, ins)
            self.ins[e].append(ins)

    def wait_all(self, eng, bufs=None):
        ins = Ins(eng, None)
        ins.idx = len(self.ins[eng])
        for t in self._all_tokens():
            if t[0] == "E" and t[1] == eng:
                continue
            self._need(eng, t, ins)
        self.ins[eng].append(ins)

    def emit(self, nc):
        nep = {}
        for e in ENGS:
            c = 0
            for ins in self.ins[e]:
                if ins.signal:
                    c += 1
                ins.epoch = max(c - 1, 0) // EPOCH_E
                ins.rank = c - ins.epoch * EPOCH_E
            nep[e] = max(c - 1, 0) // EPOCH_E + 1
        with ExitStack() as st:
            esem = {e: [st.enter_context(nc.semaphore("s_%s%d" % (e, k))) for k in range(nep[e])] for e in ENGS}
            for i, b in enumerate(self.dbufs):
                b.dsems = [st.enter_context(nc.semaphore("d%d_%d" % (i, k))) for k in range(b.depoch + 1)]
            block = st.enter_context(nc.Block())

            def run(e, handle):
                for ins in self.ins[e]:
                    for tok in ins.waits:
                        if tok[0] == "E":
                            p = self.ins[tok[1]][tok[2]]
                            handle.wait_ge(esem[tok[1]][p.epoch], p.rank)
                        else:
                            handle.wait_ge(tok[1].dsems[tok[3]], tok[2])
                    if ins.fn is None:
                        continue
                    r = ins.fn(handle)
                    if ins.dbuf is not None:
                        r.then_inc(ins.dbuf.dsems[ins.depoch], ins.inc)
                    elif ins.signal:
                        r.then_inc(esem[e][ins.epoch], 1)

            @block.tensor
            def _(h):
                run("pe", h)

            @block.scalar
            def _(h):
                run("act", h)

            @block.vector
            def _(h):
                run("dve", h)

            @block.gpsimd
            def _(h):
                run("pool", h)

            @block.sync
            def _(h):
                run("sp", h)


def fap(base, off, dims):
    return bass.AP(base.tensor, base.offset + off, [list(base.ap[0])] + [[s, c] for s, c in dims])


class Arena:
    def __init__(self, nc, lo=16640, hi=229376):
        self.nc = nc
        self.lo = lo
        self.hi = hi
        self.n = 0

    def at(self, off, shape, dt, name=None):
        self.n += 1
        nbytes = int(np.prod(shape[1:])) * (4 if dt == F32 else 2)
        assert self.lo + off + nbytes <= self.hi, (name, off, nbytes)
        return self.nc.alloc_sbuf_tensor_at("%s_%d" % (name or "t", self.n), list(shape), dt, offset=self.lo + off)


def build_program():
    nc = bass.Bass("TRN2", target_bir_lowering=False)
    dt_in = lambda name, shape: nc.dram_tensor(name, list(shape), F32, kind="ExternalInput").ap()
    x_d = dt_in("x", [NPASS, NT, D])
    norm_mix = dt_in("norm_mix", [2, D])
    s5_a_re = dt_in("s5_a_re", [128, 64])
    s5_a_im = dt_in("s5_a_im", [128, 64])
    s5_ls = dt_in("s5_log_step", [128, 1])
    s5_b_re = dt_in("s5_b_re", [128, 64 * 16])
    s5_b_im = dt_in("s5_b_im", [128, 64 * 16])
    s5_c_re = dt_in("s5_c_re", [128, 16 * 64])
    s5_c_im = dt_in("s5_c_im", [128, 16 * 64])
    s5_d = dt_in("s5_d", [D])
    w_glu = dt_in("s5_w_glu", [D, 2 * D] if not LITE else [128, 128])
    w_qkv = dt_in("attn_w_qkv", [D, QKV])
    b_qkv = dt_in("attn_b_qkv", [QKV])
    sinks = dt_in("attn_sinks", [32])
    w_o = dt_in("attn_w_o", [D, D])
    norm_ffn = dt_in("norm_ffn", [2, D])
    w_gate = dt_in("ffn_w_gate", [2, D, DFF] if not LITE else [2, 128, 128])
    w_up = dt_in("ffn_w_up", [2, D, DFF] if not LITE else [2, 128, 128])
    w_down = dt_in("ffn_w_down", [2, DFF, D] if not LITE else [2, 128, 128])
    norm_final = dt_in("norm_final", [D])
    cmask_d = dt_in("cmask", [128, 16])
    hmask_d = dt_in("hmask", [128, NPASS])
    out_d = nc.dram_tensor("out", [NPASS, NT, D], F32, kind="ExternalOutput").ap()

    A = Arena(nc)
    P = Prog()

    off = 0

    def alloc(shape, dt, name):
        nonlocal off
        t = A.at(off, shape, dt, name)
        nbytes = int(np.prod(shape[1:])) * (4 if dt == F32 else 2)
        off += (nbytes + 63) // 64 * 64
        return t

    ident_bf = alloc([128, 128], BF16, "identb")
    ident_f = alloc([128, 128], F32, "identf")
    ones_bf = alloc([128, 128], BF16, "ones")
    eps_t = alloc([128, 1], F32, "eps")
    gvec = alloc([128, 5, KC], F32, "gvec")
    bq_t = alloc([128, 20], F32, "bqkv")
    sink_t = alloc([128, 32], F32, "sinks")
    cmask = alloc([128, 16], F32, "cmask")
    hmask = alloc([128, NPASS], F32, "hmask")
    dist_t = alloc([128, 256], F32, "dist")
    dist_h = alloc([128, 256], F32, "disth")
    prev7 = alloc([128, 512], BF16, "prev7")
    bq8 = alloc([128, 20], F32, "bq8")
    bvb = alloc([128, 256], F32, "bvb")
    APW = alloc([128, 8, 2, 64], F32, "apw")
    A1K = alloc([128, 2, 64], F32, "a1k")
    SC = alloc([128, 2, 64], F32, "scar")
    HIN = alloc([128, 2, 64], F32, "hin")
    rstd_off = off
    rstd = alloc([128, NT], F32, "rstd")
    hT = alloc([128, KC, NT], F32, "hT")
    acta_off = off
    actA = alloc([128, KC, NT], BF16, "actA")
    wf_off = off
    WF = [alloc([128, KC * 128], F32, "wf%d" % i) for i in range(3)]
    WB = [alloc([128, KC * 128], BF16, "wb%d" % i) for i in range(3)]
    alloc_p = alloc


    b_const = Buf("const")
    b_gvec = Buf("gvec")
    b_rstd = Buf("rstd")
    b_hT = [Buf("hT%d" % c) for c in range(KC)]
    b_actA = [Buf("actA%d" % c) for c in range(KC)]
    b_WF = [Buf("wf%d" % i) for i in range(3)]
    b_WB = [Buf("wb%d" % i) for i in range(3)]
    b_xs = [Buf("xs%d" % i) for i in range(2)]
    b_out = Buf("out")

    PS = [nc.alloc_psum_tensor("ps%d" % i, [128, 512], F32) for i in range(8)]
    b_PS = [Buf("ps%d" % i) for i in range(8)]
    ps_rr = [0]
    PSB = PS[7][:].bitcast(BF16) if hasattr(PS[7][:], "bitcast") else None

    def next_bank(lo=0, hi=6):
        i = lo + ps_rr[0] % (hi - lo)
        ps_rr[0] += 1
        return i

    w_rr = [0]

    P.op("pool", lambda e: e.memset(ident_bf[:], 1.0), writes=[b_const])
    P.op("pool", lambda e: e.affine_select(out=ident_bf[:], in_=ident_bf[:], pattern=[[-1, 128]],
                                           compare_op=ALU.is_equal, fill=0.0, base=0, channel_multiplier=1),
         reads=[b_const], writes=[b_const])
    P.op("pool", lambda e: e.memset(ident_f[:], 1.0), writes=[b_const])
    P.op("pool", lambda e: e.affine_select(out=ident_f[:], in_=ident_f[:], pattern=[[-1, 128]],
                                           compare_op=ALU.is_equal, fill=0.0, base=0, channel_multiplier=1),
         reads=[b_const], writes=[b_const])
    P.op("pool", lambda e: e.memset(ones_bf[:], 1.0), writes=[b_const])
    P.op("pool", lambda e: e.memset(eps_t[:], EPS), writes=[b_const])
    P.op("pool", lambda e: e.iota(dist_t[:], pattern=[[-1, 256]], base=128, channel_multiplier=1,
                                  allow_small_or_imprecise_dtypes=True), writes=[b_const])
    BIG = 1.0e9
    P.op("pool", lambda e: e.affine_select(out=dist_t[:], in_=dist_t[:], pattern=[[-1, 256]],
                                           compare_op=ALU.is_ge, fill=BIG, base=128, channel_multiplier=1),
         reads=[b_const], writes=[b_const])
    P.op("pool", lambda e: e.affine_select(out=dist_t[:], in_=dist_t[:], pattern=[[1, 256]],
                                           compare_op=ALU.is_ge, fill=BIG, base=-1, channel_multiplier=-1),
         reads=[b_const], writes=[b_const])
    for i, src in enumerate([norm_mix[0], norm_ffn[0], norm_mix[1], norm_ffn[1], norm_final]):
        P.dma("sp", lambda e, i=i, src=src: e.dma_start(out=gvec[:, i, :], in_=src.rearrange("(k p) -> p k", p=128),
                                                        allow_slow_non_contiguous=True),
              writes=[b_gvec])
    P.dma("sp", lambda e: e.dma_start(out=bq_t[:], in_=b_qkv.rearrange("(k p) -> p k", p=128),
                                      allow_slow_non_contiguous=True), writes=[b_gvec])
    P.dma("sp", lambda e: e.dma_start(out=sink_t[:], in_=bass.AP(sinks.tensor, sinks.offset, [[0, 128], [1, 32]])),
          writes=[b_gvec])
    P.dma("sp", lambda e: e.dma_start(out=cmask[:], in_=cmask_d), writes=[b_gvec])
    P.dma("sp", lambda e: e.dma_start(out=hmask[:], in_=hmask_d), writes=[b_gvec])

    def load_x_to_hT(ps_):
        P.barrier()
        xs = [A.at(phase_off + i * D * 4, [128, D], F32, "xs%d" % i) for i in range(2)]
        for tt in range(NT // 128):
            s = tt % 2
            P.dma("sp", lambda e, s=s, tt=tt: e.dma_start(out=xs[s][:], in_=x_d[ps_, tt * 128:(tt + 1) * 128, :]),
                  writes=[b_xs[s]])
            for c4 in range(KC // 4):
                bk = next_bank()
                for j in range(4):
                    c = c4 * 4 + j
                    P.op("pe", lambda e, bk=bk, j=j, c=c, s=s: e.transpose(
                        out=PS[bk][:, j * 128:(j + 1) * 128], in_=xs[s][:, c * 128:(c + 1) * 128], identity=ident_f[:]),
                        reads=[b_xs[s], b_const], writes=[b_PS[bk]])
                eng = "act" if (c4 % 2 == 0) else "dve"
                dst = lambda c4=c4, tt=tt: hT[:, c4 * 4:(c4 + 1) * 4, tt * 128:(tt + 1) * 128]
                src = lambda bk=bk: PS[bk][:].rearrange("p (a b) -> p a b", a=4)
                if eng == "act":
                    P.op("act", lambda e, dst=dst, src=src: e.activation(out=dst(), in_=src(), func=AF.Copy),
                         reads=[b_PS[bk]], writes=[b_hT[c4 * 4 + j] for j in range(4)])
                else:
                    P.op("dve", lambda e, dst=dst, src=src: e.tensor_copy(out=dst(), in_=src()),
                         reads=[b_PS[bk]], writes=[b_hT[c4 * 4 + j] for j in range(4)])

    def rmsnorm_T(gi, dst, b_dst):
        sq = A.at(phase_off, [128, 2, NT], BF16, "sq")
        b_sq = [Buf("sq0"), Buf("sq1")]
        banks = [6, 7]
        for c in range(KC):
            s = c % 2
            P.op("act", lambda e, c=c, s=s: e.activation(out=sq[:, s, :], in_=hT[:, c, :], func=AF.Square),
                 reads=[b_hT[c]], writes=[b_sq[s]])
            for tb in range(2):
                P.op("pe", lambda e, c=c, s=s, tb=tb: e.matmul(PS[banks[tb]][:], lhsT=ones_bf[:],
                                                               rhs=sq[:, s, tb * 512:(tb + 1) * 512],
                                                               start=(c == 0), stop=(c == KC - 1)),
                     reads=[b_sq[s], b_const], writes=[b_PS[banks[tb]]])
        for tb in range(2):
            P.op("act", lambda e, tb=tb: e.activation(out=rstd[:, tb * 512:(tb + 1) * 512], in_=PS[banks[tb]][:],
                                                      func=AF.Sqrt, bias=eps_t[:], scale=1.0 / D),
                 reads=[b_PS[banks[tb]], b_const], writes=[b_rstd])
        P.op("dve", lambda e: e.reciprocal(out=rstd[:], in_=rstd[:]), reads=[b_rstd], writes=[b_rstd])
        for c in range(KC):
            P.op("dve", lambda e, c=c: e.scalar_tensor_tensor(out=dst[:, c, :], in0=hT[:, c, :], scalar=gvec[:, gi, c:c + 1],
                                                             in1=rstd[:], op0=ALU.mult, op1=ALU.mult),
                 reads=[b_hT[c], b_rstd, b_gvec], writes=[b_dst[c]])

    def load_w(W, kc, n0, ncols=128, cast_eng=None):
        i = w_rr[0] % 3
        w_rr[0] += 1
        wv = W.rearrange("(k p) n -> p k n", p=128)[:, :, n0:n0 + ncols]
        f32v = lambda i=i: WF[i][:, 0:kc * ncols].rearrange("p (k n) -> p k n", k=kc)
        P.dma("sp", lambda e, i=i, wv=wv, f32v=f32v: e.dma_start(out=f32v(), in_=wv), writes=[b_WF[i]])
        ce = cast_eng or "pool"
        if ce == "act":
            P.op("act", lambda e, i=i: e.activation(out=WB[i][:, 0:kc * ncols], in_=WF[i][:, 0:kc * ncols], func=AF.Copy),
                 reads=[b_WF[i]], writes=[b_WB[i]])
        else:
            P.op(ce, lambda e, i=i: e.tensor_copy(out=WB[i][:, 0:kc * ncols], in_=WF[i][:, 0:kc * ncols]),
                 reads=[b_WF[i]], writes=[b_WB[i]])
        return i

    def dense_chunk(W, kc, n0, rhs, b_rhs, evac):
        i = load_w(W, kc, n0)
        banks = [next_bank(), next_bank()]
        for k in range(kc):
            for tb in range(2):
                P.op("pe", lambda e, i=i, k=k, tb=tb, bk=banks[tb]: e.matmul(
                    PS[bk][:], lhsT=WB[i][:, k * 128:(k + 1) * 128], rhs=rhs(k, tb), start=(k == 0), stop=(k == kc - 1)),
                    reads=[b_WB[i], b_rhs[k]], writes=[b_PS[banks[tb]]])
        for tb in range(2):
            evac(tb, banks[tb])

    def ffn(layer, gi):
        P.barrier()
        rmsnorm_T(gi, actA, b_actA)
        act = A.at(phase_off + 4096, [128, FCB, NT], BF16, "ffnact")
        b_act = [Buf("ffnact%d" % j) for j in range(FCB)]
        sg = A.at(phase_off + 4096 + FCB * NT * 2, [128, 2, 512], BF16, "silu")
        b_sg = [Buf("sg0"), Buf("sg1")]
        rhsA = lambda k, tb: actA[:, k, tb * 512:(tb + 1) * 512]
        for fb in range(NFB):
            for j in range(FCB):
                f0 = (fb * FCB + j) * 128
                ig = load_w(w_gate[layer], KC, f0)
                iu = load_w(w_up[layer], KC, f0)
                for tb in range(2):
                    bg, bu = next_bank(), next_bank()
                    for k in range(KC):
                        P.op("pe", lambda e, ig=ig, k=k, tb=tb, bg=bg: e.matmul(
                            PS[bg][:], lhsT=WB[ig][:, k * 128:(k + 1) * 128], rhs=rhsA(k, tb), start=(k == 0), stop=(k == KC - 1)),
                            reads=[b_WB[ig], b_actA[k]], writes=[b_PS[bg]])
                    for k in range(KC):
                        P.op("pe", lambda e, iu=iu, k=k, tb=tb, bu=bu: e.matmul(
                            PS[bu][:], lhsT=WB[iu][:, k * 128:(k + 1) * 128], rhs=rhsA(k, tb), start=(k == 0), stop=(k == KC - 1)),
                            reads=[b_WB[iu], b_actA[k]], writes=[b_PS[bu]])
                    P.op("act", lambda e, tb=tb, bg=bg: e.activation(out=sg[:, tb, :], in_=PS[bg][:], func=AF.Silu),
                         reads=[b_PS[bg]], writes=[b_sg[tb]])
                    P.op("dve", lambda e, tb=tb, bu=bu, j=j: e.tensor_tensor(out=act[:, j, tb * 512:(tb + 1) * 512], in0=sg[:, tb, :],
                                                                      in1=PS[bu][:], op=ALU.mult),
                         reads=[b_sg[tb], b_PS[bu]], writes=[b_act[j]])
            for c in range(KC):
                def evac(tb, bk, c=c):
                    P.op("dve", lambda e, tb=tb, bk=bk, c=c: e.tensor_tensor(
                        out=hT[:, c, tb * 512:(tb + 1) * 512], in0=hT[:, c, tb * 512:(tb + 1) * 512], in1=PS[bk][:], op=ALU.add),
                        reads=[b_PS[bk], b_hT[c]], writes=[b_hT[c]])
                wd = w_down[layer][fb * FCB * 128:(fb + 1) * FCB * 128, :]
                dense_chunk(wd, FCB, c * 128, lambda k, tb: act[:, k, tb * 512:(tb + 1) * 512], b_act, evac)


    kv_in_d = nc.dram_tensor("kv_in", [128, 512], BF16).ap()
    kv_out_d = nc.dram_tensor("kv_out", [NCORES * 128, 512], BF16).ap()
    b_kvin, b_kvout = Buf("kvin"), Buf("kvout")
    b_prev7 = Buf("prev7")
    b_attc = Buf("attc")
    P.op("pool", lambda e: e.memset(prev7[:], 0.0), writes=[b_prev7])
    P.op("dve", lambda e: e.tensor_scalar(out=bq8[:], in0=bq_t[:], scalar1=0.125, scalar2=None, op0=ALU.mult),
         reads=[b_gvec], writes=[b_attc])
    P.dma("sp", lambda e: e.dma_start(out=bvb[:], in_=bass.AP(b_qkv.tensor, b_qkv.offset + 2304, [[0, 128], [1, 256]])),
          writes=[b_attc])
    SLOPES = [float(2.0 ** (-8.0 * (h + 1) / 32.0)) for h in range(32)]

    def attention(ps_):
        P.barrier()
        rmsnorm_T(2, actA, b_actA)
        po = phase_off
        qT = A.at(po, [128, KC, NT], BF16, "qT"); po += KC * NT * 2
        kT = A.at(po, [128, 2, 128 + NT], BF16, "kT"); po += 2 * (128 + NT) * 2
        kT2 = A.at(po, [128, 4, 128 + NT], BF16, "kT2"); po += 4 * (128 + NT) * 2
        Vt = A.at(po, [128, 9, 256], BF16, "Vt"); po += 9 * 256 * 2
        gath = WF[0][:].bitcast(BF16).rearrange("p (r c) -> p r c", r=8)
        halo = A.at(po, [128, 512], BF16, "halo"); po += 512 * 2
        sb = WF[2][:].rearrange("p (a b) -> p a b", a=8)
        pf = WF[1][:].rearrange("p (a b) -> p a b", a=8)
        pn = A.at(po, [128, 8, 256], BF16, "pn"); po += 4096
        pT = A.at(po, [128, 16, 128], BF16, "pT"); po += 4096
        st = A.at(po, [128, 64], F32, "st"); po += 256
        dh = A.at(po, [128, 256], F32, "dh"); po += 1024
        b_qT = [Buf("qT%d" % c) for c in range(KC)]
        b_kT, b_kT2, b_Vt, b_gath, b_halo = Buf("kT"), Buf("kT2"), Buf("Vt"), b_WF[0], Buf("halo")
        b_sb, b_pf, b_pn, b_pT, b_st, b_dh = b_WF[2], b_WF[1], Buf("pn"), Buf("pT"), Buf("st"), Buf("dh")
        rhsA = lambda k, tb: actA[:, k, tb * 512:(tb + 1) * 512]
        P.op("dve", lambda e: e.tensor_copy(out=dh[:], in_=dist_t[:]), reads=[b_const], writes=[b_dh])
        P.op("dve", lambda e: e.tensor_scalar(out=dh[:, 0:128], in0=dist_t[:, 0:128], scalar1=hmask[:, ps_:ps_ + 1], scalar2=None,
                                              op0=ALU.add), reads=[b_const, b_gvec, b_dh], writes=[b_dh])
        for c in range(KC):
            def evq(tb, bk, c=c):
                P.op("act", lambda e, tb=tb, bk=bk, c=c: e.activation(out=qT[:, c, tb * 512:(tb + 1) * 512], in_=PS[bk][:], func=AF.Identity,
                                                                      bias=bq8[:, c:c + 1], scale=0.125),
                     reads=[b_PS[bk], b_attc], writes=[b_qT[c]])
            dense_chunk(w_qkv, KC, c * 128, rhsA, b_actA, evq)
        for c in range(2):
            def evk(tb, bk, c=c):
                P.op("act", lambda e, tb=tb, bk=bk, c=c: e.activation(out=kT[:, c, 128 + tb * 512:128 + (tb + 1) * 512], in_=PS[bk][:],
                                                                      func=AF.Identity, bias=bq_t[:, 16 + c:17 + c], scale=1.0),
                     reads=[b_PS[bk], b_gvec], writes=[b_kT])
            dense_chunk(w_qkv, KC, 2048 + c * 128, rhsA, b_actA, evk)
        iv = [load_w(w_qkv, KC, 2304), load_w(w_qkv, KC, 2432)]
        for tt in range(NT // 128):
            bk = next_bank()
            for half in range(2):
                for k in range(KC):
                    P.op("pe", lambda e, k=k, tt=tt, half=half, bk=bk: e.matmul(
                        PS[bk][:, half * 128:(half + 1) * 128], lhsT=actA[:, k, tt * 128:(tt + 1) * 128],
                        rhs=WB[iv[half]][:, k * 128:(k + 1) * 128], start=(k == 0), stop=(k == KC - 1)),
                        reads=[b_WB[iv[half]], b_actA[k]], writes=[b_PS[bk]])
            P.op("dve", lambda e, tt=tt, bk=bk: e.tensor_tensor(out=Vt[:, 1 + tt, :], in0=PS[bk][:, 0:256], in1=bvb[:], op=ALU.add),
                 reads=[b_PS[bk], b_attc], writes=[b_Vt])
        if ATT_LEVEL < 2:
            return
        P.op("pool", lambda e: e.tensor_copy(out=halo[:, 0:256].rearrange("p (a b) -> p a b", a=2), in_=kT[:, :, NT:NT + 128]),
             reads=[b_kT], writes=[b_halo])
        P.op("pool", lambda e: e.tensor_copy(out=halo[:, 256:512], in_=Vt[:, 8, :]), reads=[b_Vt, b_halo], writes=[b_halo])
        P.dma("pool", lambda e: e.dma_start(out=kv_in_d, in_=halo[:]), reads=[b_halo], writes=[b_kvin])
        if not NOCC:
            P.dma("pool", lambda e: e.collective_compute("AllGather", ALU.bypass, replica_groups=[list(range(NCORES))],
                                                         ins=[kv_in_d.opt()], outs=[kv_out_d.opt()]),
                  reads=[b_kvin], writes=[b_kvout], inc=1)
        P.dma("pool", lambda e: e.dma_start(out=gath, in_=kv_out_d.rearrange("(r p) c -> p r c", p=128)),
              reads=[b_kvout], writes=[b_gath])
        P.op("dve", lambda e: e.tensor_scalar(out=halo[:], in0=prev7[:], scalar1=cmask[:, 0:1], scalar2=None, op0=ALU.mult),
             reads=[b_prev7, b_gvec, b_halo], writes=[b_halo])
        for j in range(1, 8):
            P.op("dve", lambda e, j=j: e.scalar_tensor_tensor(out=halo[:], in0=gath[:, j - 1, :], scalar=cmask[:, j:j + 1], in1=halo[:],
                                                              op0=ALU.mult, op1=ALU.add),
                 reads=[b_gath, b_gvec, b_halo], writes=[b_halo])
        P.op("dve", lambda e: e.tensor_copy(out=prev7[:], in_=gath[:, 7, :]), reads=[b_gath], writes=[b_prev7])
        P.op("dve", lambda e: e.tensor_copy(out=kT[:, :, 0:128], in_=halo[:, 0:256].rearrange("p (a b) -> p a b", a=2)),
             reads=[b_halo], writes=[b_kT])
        P.op("dve", lambda e: e.tensor_copy(out=Vt[:, 0, :], in_=halo[:, 256:512]), reads=[b_halo], writes=[b_Vt])
        for kvh in range(4):
            src = kT[64 * (kvh % 2):64 * (kvh % 2) + 64, kvh // 2, :]
            for hh in range(2):
                P.dma("sp", lambda e, kvh=kvh, hh=hh, src=src: e.dma_start(out=kT2[64 * hh:64 * hh + 64, kvh, :], in_=src),
                      reads=[b_kT], writes=[b_kT2])
        if ATT_LEVEL < 3:
            return
        PSB6 = PS[6][:].bitcast(BF16)
        def group(tt, kvh):
            dsel = dh if tt == 0 else dist_t
            b_dsel = b_dh if tt == 0 else b_const
            c0 = kvh * 4
            bkS = [[next_bank(), next_bank()], [next_bank(), next_bank()]]
            for pr in range(4):
                for hh in range(2):
                    bk = bkS[hh][pr // 2]
                    P.op("pe", lambda e, pr=pr, hh=hh, bk=bk: e.matmul(
                        PS[bk][:, (pr % 2) * 256:(pr % 2 + 1) * 256], lhsT=qT[64 * hh:64 * hh + 64, c0 + pr, tt * 128:(tt + 1) * 128],
                        rhs=kT2[64 * hh:64 * hh + 64, kvh, tt * 128:tt * 128 + 256], start=True, stop=True),
                        reads=[b_qT[c0 + pr], b_kT2], writes=[b_PS[bk]])
            for pr in range(4):
                for hh in range(2):
                    j = pr * 2 + hh
                    h = 2 * c0 + j
                    bk = bkS[hh][pr // 2]
                    P.op("dve", lambda e, pr=pr, j=j, h=h, bk=bk: e.scalar_tensor_tensor(
                        out=sb[:, j, :], in0=dsel[:], scalar=-SLOPES[h], in1=PS[bk][:, (pr % 2) * 256:(pr % 2 + 1) * 256],
                        op0=ALU.mult, op1=ALU.add), reads=[b_PS[bk], b_dsel], writes=[b_sb])
            h0 = 2 * c0
            P.op("dve", lambda e: e.tensor_reduce(out=st[:, 0:8], in_=sb, axis=AX.X, op=ALU.max), reads=[b_sb], writes=[b_st])
            P.op("dve", lambda e: e.tensor_tensor(out=st[:, 8:16], in0=st[:, 0:8], in1=sink_t[:, h0:h0 + 8], op=ALU.max),
                 reads=[b_st, b_gvec], writes=[b_st])
            P.op("dve", lambda e: e.tensor_scalar(out=st[:, 16:24], in0=st[:, 8:16], scalar1=-1.0, scalar2=None, op0=ALU.mult),
                 reads=[b_st], writes=[b_st])
            P.op("dve", lambda e: e.tensor_tensor(out=st[:, 32:40], in0=sink_t[:, h0:h0 + 8], in1=st[:, 8:16], op=ALU.subtract),
                 reads=[b_st, b_gvec], writes=[b_st])
            for j in range(8):
                P.op("act", lambda e, j=j: e.activation(out=pf[:, j, :], in_=sb[:, j, :], func=AF.Exp, bias=st[:, 16 + j:17 + j],
                                                        scale=1.0, accum_out=st[:, 24 + j:25 + j]),
                     reads=[b_sb, b_st], writes=[b_pf, b_st])
            P.op("act", lambda e: e.activation(out=st[:, 32:40], in_=st[:, 32:40], func=AF.Exp), reads=[b_st], writes=[b_st])
            P.op("dve", lambda e: e.tensor_tensor(out=st[:, 40:48], in0=st[:, 24:32], in1=st[:, 32:40], op=ALU.add), reads=[b_st], writes=[b_st])
            P.op("dve", lambda e: e.reciprocal(out=st[:, 48:56], in_=st[:, 40:48]), reads=[b_st], writes=[b_st])
            P.op("dve", lambda e: e.tensor_tensor(out=pn[:], in0=pf, in1=fap(st[:, 48:56], 0, [(1, 8), (0, 256)]), op=ALU.mult),
                 reads=[b_pf, b_st], writes=[b_pn])
            for half in range(2):
                PSBx = PSB6 if half == 0 else PSB
                bx = 6 + half
                for jj in range(8):
                    jb = half * 8 + jj
                    P.op("pe", lambda e, jb=jb, jj=jj, PSBx=PSBx: e.transpose(
                        out=PSBx[:, jj * 128:(jj + 1) * 128], in_=pn[:, jb // 2, (jb % 2) * 128:(jb % 2 + 1) * 128], identity=ident_bf[:]),
                        reads=[b_pn, b_const], writes=[b_PS[bx]])
                P.op("act", lambda e, half=half, PSBx=PSBx: e.activation(out=pT[:, half * 8:(half + 1) * 8, :].rearrange("p a b -> p (a b)"),
                                                                        in_=PSBx[:, 0:1024], func=AF.Copy),
                     reads=[b_PS[bx]], writes=[b_pT])
            bo = next_bank()
            for pr in range(4):
                for hh in range(2):
                    j = pr * 2 + hh
                    for blk in range(2):
                        P.op("pe", lambda e, pr=pr, hh=hh, j=j, blk=blk, bo=bo: e.matmul(
                            PS[bo][64 * hh:64 * hh + 64, pr * 128:(pr + 1) * 128], lhsT=Vt[:, tt + blk, kvh * 64:(kvh + 1) * 64],
                            rhs=pT[:, j * 2 + blk, :], start=(blk == 0), stop=(blk == 1), tile_position=(0, 64 * hh)),
                            reads=[b_Vt, b_pT], writes=[b_PS[bo]])
            P.op("act", lambda e, bo=bo: e.activation(out=actA[:, c0:c0 + 4, tt * 128:(tt + 1) * 128],
                                                      in_=PS[bo][:].rearrange("p (a b) -> p a b", a=4), func=AF.Copy),
                 reads=[b_PS[bo]], writes=[b_actA[c0 + i_] for i_ in range(4)])
        for tt_ in range(min(NT // 128, CORE_T)):
            for kvh_ in range(4):
                group(tt_, kvh_)
        if ATT_LEVEL < 4:
            return
        for c in range(KC):
            def evo(tb, bk, c=c):
                P.op("dve", lambda e, tb=tb, bk=bk, c=c: e.tensor_tensor(
                    out=hT[:, c, tb * 512:(tb + 1) * 512], in0=hT[:, c, tb * 512:(tb + 1) * 512], in1=PS[bk][:], op=ALU.add),
                    reads=[b_PS[bk], b_hT[c]], writes=[b_hT[c]])
            dense_chunk(w_o, KC, c * 128, rhsA, b_actA, evo)

    win_d = nc.dram_tensor("s5win", [128, 128, 128], BF16).ap()
    wore_d = nc.dram_tensor("s5wore", [128, 64, 128], BF16).ap()
    woim_d = nc.dram_tensor("s5woim", [128, 64, 128], BF16).ap()
    m0_d = nc.dram_tensor("s5m0", [128, 128, 128], BF16).ap()
    st_in_d = nc.dram_tensor("st_in", [128, 128], F32).ap()
    st_out_d = nc.dram_tensor("st_out", [NCORES * 128, 128], F32).ap()
    b_wind, b_wored, b_woimd, b_m0d, b_stin, b_stout = [Buf(n) for n in "wind wored woimd m0d stin stout".split()]
    b_apw, b_sc, b_hin = Buf("apw"), Buf("sc"), Buf("hin")
    PSBF = [PS[i][:].bitcast(BF16) for i in range(8)]
    X_OFF = rstd_off

    def cmul(eng, o_re, o_im, a_re, a_im, b_re_, b_im_, t1, t2, rd, wr):
        P.op(eng, lambda e: e.tensor_tensor(out=t1, in0=a_re, in1=b_re_, op=ALU.mult), reads=rd, writes=wr)
        P.op(eng, lambda e: e.tensor_tensor(out=t2, in0=a_im, in1=b_im_, op=ALU.mult), reads=rd, writes=wr)
        P.op(eng, lambda e: e.tensor_tensor(out=t1, in0=t1, in1=t2, op=ALU.subtract), reads=rd, writes=wr)
        P.op(eng, lambda e: e.tensor_tensor(out=t2, in0=a_re, in1=b_im_, op=ALU.mult), reads=rd, writes=wr)
        P.op(eng, lambda e: e.tensor_tensor(out=o_im, in0=a_im, in1=b_re_, op=ALU.mult), reads=rd, writes=wr)
        P.op(eng, lambda e: e.tensor_tensor(out=o_im, in0=o_im, in1=t2, op=ALU.add), reads=rd, writes=wr)
        P.op(eng, lambda e: e.tensor_copy(out=o_re, in_=t1), reads=rd, writes=wr)

    def s5_prologue():
        P.barrier()
        base = rstd_off
        po = [base]

        def T(n, dt=F32, parts=64, name="pt"):
            t = A.at(po[0], [parts, n], dt, name)
            po[0] += (n * (4 if dt == F32 else 2) + 63) // 64 * 64
            return t
        bp = Buf("s5p")
        ARE, AIM = T(128), T(128)
        LS, DTv = T(2), T(2)
        XR, ANG, MAG, X16, Z, SN, CS, t1, t2, t3 = [T(128) for _ in range(10)]
        LBR, LBI, NRE, DEN, QRE, QIM, LIR, LII = [T(128) for _ in range(8)]
        PR, PI = T(2 * 17 * 64), T(2 * 17 * 64)
        big_off = po[0]
        BRE, BIM, CRE, CIM = T(2048), T(2048), T(2048), T(2048)
        BBR, BBI = T(2048), T(2048)
        TA, TB = T(1024), T(1024)
        OB = T(2 * 8192, BF16)
        WoR = T(64 * 128, BF16, 128, "wor")
        WoI = T(64 * 128, BF16, 128, "woi")
        VsR = T(64 * 128, BF16, 128, "vsr")
        VsI = T(64 * 128, BF16, 128, "vsi")
        BIGB = A.at(big_off, [128, 128 * 128], BF16, "bigb")
        MASK = T(128, F32, 128, "mask")
        DT_ = T(128, F32, 128, "dt_")
        b_OB, b_WoR, b_WoI, b_VsR, b_VsI, b_BIG, b_mask = [Buf(n) for n in "OB WoR WoI VsR VsI BIG mask".split()]

        def ld(dst, src_ap):
            P.dma("sp", lambda e: e.dma_start(out=dst[:], in_=src_ap), writes=[bp])
        ld(ARE, s5_a_re.rearrange("(a b) p -> a (b p)", b=2))
        ld(AIM, s5_a_im.rearrange("(a b) p -> a (b p)", b=2))
        ld(LS, s5_ls.rearrange("(a b) o -> a (b o)", b=2))
        ld(BRE, s5_b_re.rearrange("(a b) x -> a (b x)", b=2))
        ld(BIM, s5_b_im.rearrange("(a b) x -> a (b x)", b=2))
        ld(CRE, s5_c_re.rearrange("(a b) x -> a (b x)", b=2))
        ld(CIM, s5_c_im.rearrange("(a b) x -> a (b x)", b=2))

        def dv(fn):
            P.op("dve", fn, reads=[bp], writes=[bp])

        def tt(o, a, b, op):
            dv(lambda e: e.tensor_tensor(out=o, in0=a, in1=b, op=op))

        def ts(o, a, s1, op0, s2=None, op1=None):
            if op1 is None:
                dv(lambda e: e.tensor_scalar(out=o, in0=a, scalar1=s1, scalar2=None, op0=op0))
            else:
                dv(lambda e: e.tensor_scalar(out=o, in0=a, scalar1=s1, scalar2=s2, op0=op0, op1=op1))
        import math

        def polyv(out, var, coefs):
            ts(out, var, float(coefs[-1]), ALU.mult)
            for a in reversed(coefs[1:-1]):
                dv(lambda e, a=a: e.scalar_tensor_tensor(out=out, in0=out, scalar=float(a), in1=var, op0=ALU.add, op1=ALU.mult))
            ts(out, out, float(coefs[0]), ALU.add)
        EXPC = [1.0 / math.factorial(k) for k in range(9)]
        ts(LS[:], LS[:], 1.0 / 64.0, ALU.mult)
        polyv(DTv[:], LS[:], EXPC)
        for _ in range(6):
            tt(DTv[:], DTv[:], DTv[:], ALU.mult)
        dtb = fap(DTv[:], 0, [(1, 2), (0, 64)])
        v3 = lambda t: fap(t[:], 0, [(64, 2), (1, 64)])
        tt(v3(XR), v3(ARE), dtb, ALU.mult)
        tt(v3(ANG), v3(AIM), dtb, ALU.mult)
        polyv(MAG[:], XR[:], EXPC)
        ts(X16[:], ANG[:], 1.0 / 16.0, ALU.mult)
        tt(Z[:], X16[:], X16[:], ALU.mult)

        def poly(out, coefs):
            ts(out[:], Z[:], coefs[-1], ALU.mult)
            for a in reversed(coefs[1:-1]):
                dv(lambda e, a=a: e.scalar_tensor_tensor(out=out[:], in0=out[:], scalar=float(a), in1=Z[:], op0=ALU.add, op1=ALU.mult))
            ts(out[:], out[:], float(coefs[0]), ALU.add)
        import math
        poly(SN, [(-1.0) ** k / math.factorial(2 * k + 1) for k in range(7)])
        tt(SN[:], SN[:], X16[:], ALU.mult)
        poly(CS, [(-1.0) ** k / math.factorial(2 * k) for k in range(8)])
        for _ in range(4):
            tt(t1[:], CS[:], SN[:], ALU.mult)
            tt(t2[:], CS[:], CS[:], ALU.mult)
            tt(t3[:], SN[:], SN[:], ALU.mult)
            tt(CS[:], t2[:], t3[:], ALU.subtract)
            ts(SN[:], t1[:], 2.0, ALU.mult)
        tt(LBR[:], MAG[:], CS[:], ALU.mult)
        tt(LBI[:], MAG[:], SN[:], ALU.mult)
        ts(NRE[:], LBR[:], -1.0, ALU.add)
        tt(t1[:], ARE[:], ARE[:], ALU.mult)
        tt(t2[:], AIM[:], AIM[:], ALU.mult)
        tt(DEN[:], t1[:], t2[:], ALU.add)
        dv(lambda e: e.reciprocal(out=DEN[:], in_=DEN[:]))
        tt(t1[:], NRE[:], ARE[:], ALU.mult)
        tt(t2[:], LBI[:], AIM[:], ALU.mult)
        tt(t1[:], t1[:], t2[:], ALU.add)
        tt(QRE[:], t1[:], DEN[:], ALU.mult)
        tt(t1[:], LBI[:], ARE[:], ALU.mult)
        tt(t2[:], NRE[:], AIM[:], ALU.mult)
        tt(t1[:], t1[:], t2[:], ALU.subtract)
        tt(QIM[:], t1[:], DEN[:], ALU.mult)
        qb = lambda q: fap(q[:], 0, [(64, 2), (1, 64), (0, 16)])
        b4 = lambda t: fap(t[:], 0, [(1024, 2), (16, 64), (1, 16)])
        tt(b4(BBR), b4(BRE), qb(QRE), ALU.mult)
        tt(b4(TA) if False else b4(BBI), b4(BIM), qb(QIM), ALU.mult)
        tt(b4(BBR), b4(BBR), b4(BBI), ALU.subtract)
        tt(b4(BBI), b4(BIM), qb(QRE), ALU.mult)
        tt(b4(BRE), b4(BRE), qb(QIM), ALU.mult)
        tt(b4(BBI), b4(BBI), b4(BRE), ALU.add)
        tt(t1[:], LBR[:], LBR[:], ALU.mult)
        tt(t2[:], LBI[:], LBI[:], ALU.mult)
        tt(t1[:], t1[:], t2[:], ALU.add)
        dv(lambda e: e.reciprocal(out=t1[:], in_=t1[:]))
        tt(LIR[:], LBR[:], t1[:], ALU.mult)
        tt(LII[:], LBI[:], t1[:], ALU.mult)
        ts(LII[:], LII[:], -1.0, ALU.mult)
        pw = lambda t, j: fap(t[:], (j + 8) * 64, [(17 * 64, 2), (1, 64)])
        dv(lambda e: e.memset(pw(PR, 0), 1.0))
        dv(lambda e: e.memset(pw(PI, 0), 0.0))
        for j in range(0, 8):
            tt(v3(t1), pw(PR, j), v3(LBR), ALU.mult)
            tt(v3(t2), pw(PI, j), v3(LBI), ALU.mult)
            tt(pw(PR, j + 1), v3(t1), v3(t2), ALU.subtract)
            tt(v3(t1), pw(PR, j), v3(LBI), ALU.mult)
            tt(v3(t2), pw(PI, j), v3(LBR), ALU.mult)
            tt(pw(PI, j + 1), v3(t1), v3(t2), ALU.add)
        for j in range(0, -8, -1):
            tt(v3(t1), pw(PR, j), v3(LIR), ALU.mult)
            tt(v3(t2), pw(PI, j), v3(LII), ALU.mult)
            tt(pw(PR, j - 1), v3(t1), v3(t2), ALU.subtract)
            tt(v3(t1), pw(PR, j), v3(LII), ALU.mult)
            tt(v3(t2), pw(PI, j), v3(LIR), ALU.mult)
            tt(pw(PI, j - 1), v3(t1), v3(t2), ALU.add)
        for ri, (src, tmp_) in enumerate(((PR, t1), (PI, t2))):
            tt(v3(tmp_), pw(src, 8), pw(src, 8), ALU.max)
            P.op("pe", lambda e, tmp_=tmp_, ri=ri: e.transpose(out=PS[0][:, ri * 64:(ri + 1) * 64], in_=tmp_[:], identity=ident_f[0:64, 0:64]),
                 reads=[bp, b_const], writes=[b_PS[0]])
        P.op("act", lambda e: e.activation(out=APW[:, 0, :, :], in_=PS[0][:, 0:128].rearrange("p (a b) -> p a b", a=2), func=AF.Copy),
             reads=[b_PS[0]], writes=[b_apw])
        s1 = T(64, F32, 128, "s1")
        s2 = T(64, F32, 128, "s2")
        for l in range(1, 8):
            a_r, a_i = APW[:, l - 1, 0, :], APW[:, l - 1, 1, :]
            cmul("dve", APW[:, l, 0, :], APW[:, l, 1, :], a_r, a_i, a_r, a_i, s1[:], s2[:], [b_apw, bp], [b_apw, bp])
        P.op("dve", lambda e: e.tensor_copy(out=A1K[:], in_=APW[:, 7, :, :]), reads=[b_apw], writes=[b_apw])
        P.op("dve", lambda e: e.memset(SC[:], 0.0), writes=[b_sc])

        def build_rows(dstR, b_dst, kind, which):
            for g2 in range(2):
                for rr in range(8):
                    if kind == "wo":
                        pr = fap(PR[:], g2 * 17 * 64 + (rr + 1 + 8) * 64, [(0, 16), (1, 64)])
                        pi = fap(PI[:], g2 * 17 * 64 + (rr + 1 + 8) * 64, [(0, 16), (1, 64)])
                        xr = fap(CRE[:], g2 * 1024, [(64, 16), (1, 64)])
                        xi = fap(CIM[:], g2 * 1024, [(64, 16), (1, 64)])
                    else:
                        pr = fap(PR[:], g2 * 17 * 64 + (-(rr + 1) + 8) * 64, [(0, 16), (1, 64)])
                        pi = fap(PI[:], g2 * 17 * 64 + (-(rr + 1) + 8) * 64, [(0, 16), (1, 64)])
                        xr = fap(BBR[:], g2 * 1024, [(1, 16), (16, 64)])
                        xi = fap(BBI[:], g2 * 1024, [(1, 16), (16, 64)])
                    ta = fap(TA[:], 0, [(64, 16), (1, 64)])
                    tb_ = fap(TB[:], 0, [(64, 16), (1, 64)])
                    ob = fap(OB[:], rr * 16 * 128 + g2 * 64, [(128, 16), (1, 64)])
                    if which == "re":
                        tt(ta, pr, xr, ALU.mult)
                        tt(tb_, pi, xi, ALU.mult)
                        P.op("dve", lambda e, ob=ob, ta=ta, tb_=tb_: e.tensor_tensor(out=ob, in0=ta, in1=tb_, op=ALU.subtract),
                             reads=[bp], writes=[b_OB, bp])
                    else:
                        tt(ta, pr, xi, ALU.mult)
                        tt(tb_, pi, xr, ALU.mult)
                        if kind == "wo":
                            P.op("dve", lambda e, ob=ob, ta=ta, tb_=tb_: e.scalar_tensor_tensor(out=ob, in0=ta, scalar=-1.0, in1=tb_,
                                                                                               op0=ALU.mult, op1=ALU.subtract),
                                 reads=[bp], writes=[b_OB, bp])
                        else:
                            P.op("dve", lambda e, ob=ob, ta=ta, tb_=tb_: e.tensor_tensor(out=ob, in0=ta, in1=tb_, op=ALU.add),
                                 reads=[bp], writes=[b_OB, bp])
            for q0 in range(0, 128, 16):
                bk = 6 + (q0 // 16) % 2
                for q in range(q0, q0 + 16):
                    P.op("pe", lambda e, q=q, bk=bk, q0=q0: e.transpose(out=PSBF[bk][:, (q - q0) * 64:(q - q0 + 1) * 64],
                                                                     in_=OB[:, q * 128:(q + 1) * 128], identity=ident_bf[0:64, 0:64]),
                         reads=[b_OB, b_const], writes=[b_PS[bk]])
                P.op("act", lambda e, bk=bk, q0=q0: e.activation(out=fap(dstR[:], q0, [(128, 64), (1, 16)]),
                                                                 in_=fap(PSBF[bk], 0, [(1, 64), (64, 16)]), func=AF.Copy),
                     reads=[b_PS[bk]], writes=[b_dst])
        build_rows(WoR, b_WoR, "wo", "re")
        build_rows(WoI, b_WoI, "wo", "im")
        build_rows(VsR, b_VsR, "v", "re")
        build_rows(VsI, b_VsI, "v", "im")
        P.dma("sp", lambda e: e.dma_start(out=wore_d, in_=WoR[:].rearrange("p (a b) -> p a b", a=64)), reads=[b_WoR], writes=[b_wored])
        P.dma("sp", lambda e: e.dma_start(out=woim_d, in_=WoI[:].rearrange("p (a b) -> p a b", a=64)), reads=[b_WoI], writes=[b_woimd])
        P.op("pool", lambda e: e.memset(MASK[:], 1.0), writes=[b_mask])
        P.op("pool", lambda e: e.affine_select(out=MASK[:].rearrange("p (a b) -> p a b", a=8), in_=MASK[:].rearrange("p (a b) -> p a b", a=8),
                                               pattern=[[16, 8], [0, 16]], compare_op=ALU.is_ge, fill=0.0, base=15, channel_multiplier=-1),
             reads=[b_mask], writes=[b_mask])
        for s_ in range(8):
            P.dma("sp", lambda e, s_=s_: e.dma_start(out=DT_[16 * s_:16 * s_ + 16, :], in_=s5_d.rearrange("(g c) -> c g", c=16),
                                                     allow_slow_non_contiguous=True), writes=[b_mask])
        for gp0 in range(0, 64, 4):
            bks = [next_bank(), next_bank()]
            for gpl in range(4):
                gp = gp0 + gpl
                for g2 in range(2):
                    for ri, (vs, wo) in enumerate(((VsR, WoR), (VsI, WoI))):
                        P.op("pe", lambda e, gpl=gpl, gp=gp, g2=g2, vs=vs, wo=wo, ri=ri, bk=bks[g2]: e.matmul(
                            PS[bk][:, gpl * 128:(gpl + 1) * 128], lhsT=vs[64 * g2:64 * g2 + 64, gp * 128:(gp + 1) * 128],
                            rhs=wo[64 * g2:64 * g2 + 64, gp * 128:(gp + 1) * 128], start=(ri == 0), stop=(ri == 1)),
                            reads=[b_VsR, b_VsI, b_WoR, b_WoI], writes=[b_PS[bks[g2]]])
            for g2 in range(2):
                P.op("dve", lambda e, gp0=gp0, g2=g2, bk=bks[g2]: e.tensor_tensor(
                    out=fap(BIGB[:], (2 * gp0 + g2) * 128, [(256, 4), (1, 128)]),
                    in0=PS[bk][:].rearrange("p (a b) -> p a b", a=4),
                    in1=fap(MASK[:], 0, [(0, 4), (1, 128)]), op=ALU.mult),
                    reads=[b_PS[bks[g2]], b_mask], writes=[b_BIG])
        for g in range(128):
            P.op("dve", lambda e, g=g: e.scalar_tensor_tensor(out=BIGB[:, g * 128:(g + 1) * 128], in0=ident_bf[:], scalar=DT_[:, g:g + 1],
                                                               in1=BIGB[:, g * 128:(g + 1) * 128], op0=ALU.mult, op1=ALU.add),
                 reads=[b_BIG, b_mask, b_const], writes=[b_BIG])
        P.dma("sp", lambda e: e.dma_start(out=m0_d, in_=BIGB[:].rearrange("p (a b) -> p a b", a=128)), reads=[b_BIG], writes=[b_m0d])
        WIB = OB
        for g2 in range(2):
            for ri in range(2):
                for s_ in range(8):
                    pr = fap(PR[:], g2 * 17 * 64 + (7 - s_ + 8) * 64, [(1, 64), (0, 16)])
                    pi = fap(PI[:], g2 * 17 * 64 + (7 - s_ + 8) * 64, [(1, 64), (0, 16)])
                    xr = fap(BBR[:], g2 * 1024, [(16, 64), (1, 16)])
                    xi = fap(BBI[:], g2 * 1024, [(16, 64), (1, 16)])
                    ta = fap(TA[:], 0, [(16, 64), (1, 16)])
                    tb_ = fap(TB[:], 0, [(16, 64), (1, 16)])
                    ob = fap(WIB[:], s_ * 16, [(128, 64), (1, 16)])
                    if ri == 0:
                        tt(ta, pr, xr, ALU.mult)
                        tt(tb_, pi, xi, ALU.mult)
                        P.op("dve", lambda e, ob=ob, ta=ta, tb_=tb_: e.tensor_tensor(out=ob, in0=ta, in1=tb_, op=ALU.subtract),
                             reads=[bp], writes=[b_OB, bp])
                    else:
                        tt(ta, pr, xi, ALU.mult)
                        tt(tb_, pi, xr, ALU.mult)
                        P.op("dve", lambda e, ob=ob, ta=ta, tb_=tb_: e.tensor_tensor(out=ob, in0=ta, in1=tb_, op=ALU.add),
                             reads=[bp], writes=[b_OB, bp])
                for p0 in range(0, 64, 8):
                    bk = 6 + (p0 // 8) % 2
                    for pp in range(p0, p0 + 8):
                        P.op("pe", lambda e, pp=pp, bk=bk, p0=p0: e.transpose(out=PSBF[bk][:, (pp - p0) * 64:(pp - p0 + 1) * 64],
                                                                           in_=WIB[:, pp * 128:(pp + 1) * 128], identity=ident_bf[0:64, 0:64]),
                             reads=[b_OB, b_const], writes=[b_PS[bk]])
                    P.op("act", lambda e, bk=bk, p0=p0, g2=g2, ri=ri: e.activation(
                        out=fap(BIGB[:], g2 * 128 + ri * 64 + p0, [(256, 64), (1, 8)]),
                        in_=fap(PSBF[bk], 0, [(1, 64), (64, 8)]), func=AF.Copy),
                        reads=[b_PS[bk]], writes=[b_BIG])
        P.dma("sp", lambda e: e.dma_start(out=win_d, in_=BIGB[:].rearrange("p (a b) -> p a b", a=128)), reads=[b_BIG], writes=[b_wind])
        P.barrier()

    def s5_pass(ps_):
        P.barrier()
        po = phase_off
        Up = A.at(po, [128, 128, 128], BF16, "Up"); po += 32768
        xs = [A.at(po + i * 8192, [128, D], F32, "s5xs%d" % i) for i in range(2)]
        gb = A.at(po + 16384, [128, D], F32, "gb")
        stg0 = po
        po += 24576
        ssq = A.at(po, [128, 8], F32, "ssq"); po += 64
        rs = A.at(po, [128, 8], F32, "rs"); po += 64
        t1 = A.at(po, [128, 64], F32, "t1"); po += 256
        t2 = A.at(po, [128, 64], F32, "t2"); po += 256
        t3 = A.at(po, [128, 64], F32, "t3"); po += 256
        t4 = A.at(po, [128, 64], F32, "t4"); po += 256
        EE = A.at(po, [128, 2, 64], F32, "EE"); po += 512
        gath = actA[:, 0, :].bitcast(F32).rearrange("p (a b) -> p a b", a=8) if False else A.at(acta_off, [128, 8, 128], F32, "sgath")
        u = actA[:].rearrange("p c t -> p (c t)")
        Xre = A.at(X_OFF, [128, 64, 129], F32, "Xre")
        Xim = A.at(X_OFF + 64 * 129 * 4, [128, 64, 129], F32, "Xim")
        TAr = A.at(wf_off, [128, 64, 64], F32, "TAr")
        TAi = A.at(wf_off + 16384, [128, 64, 64], F32, "TAi")
        TBr = A.at(stg0, [128, 64, 32], F32, "TBr")
        TBi = A.at(stg0 + 8192, [128, 64, 32], F32, "TBi")
        TCr = A.at(stg0 + 16384, [128, 64, 16], F32, "TCr")
        TCi = A.at(stg0 + 16384 + 4096, [128, 64, 16], F32, "TCi")
        b_Up = [Buf("Up%d" % i) for i in range(16)]
        b_xs2 = [Buf("s5xs0"), Buf("s5xs1")]
        b_gb, b_ss, b_X, b_tr, b_EE, b_sg = Buf("gb"), Buf("ssq"), Buf("X"), Buf("tree"), Buf("EE"), Buf("sgath")
        xv = x_d[ps_].rearrange("(k r) d -> k r d", r=8)
        P.dma("sp", lambda e: e.dma_start(out=gb[:], in_=bass.AP(norm_mix.tensor, norm_mix.offset, [[0, 128], [1, D]])), writes=[b_gb])
        junk = Up[:, 0:16, :].rearrange("p a b -> p (a b)")
        for r in range(8):
            s = r % 2
            P.dma("sp", lambda e, s=s, r=r: e.dma_start(out=xs[s][:], in_=xv[:, r, :]), writes=[b_xs2[s]])
            P.op("act", lambda e, s=s, r=r: e.activation(out=junk, in_=xs[s][:], func=AF.Square, accum_out=ssq[:, r:r + 1]),
                 reads=[b_xs2[s]], writes=[b_Up[0], b_ss])
            P.op("act", lambda e, r=r: e.activation(out=rs[:, r:r + 1], in_=ssq[:, r:r + 1], func=AF.Sqrt, bias=eps_t[:], scale=1.0 / D),
                 reads=[b_ss, b_const], writes=[b_ss])
            P.op("dve", lambda e, r=r: e.reciprocal(out=rs[:, r:r + 1], in_=rs[:, r:r + 1]), reads=[b_ss], writes=[b_ss])
            P.op("dve", lambda e, s=s, r=r: e.scalar_tensor_tensor(out=fap(u, r * 16, [(128, 128), (1, 16)]),
                                                                   in0=xs[s][:].rearrange("p (g c) -> p g c", c=16), scalar=rs[:, r:r + 1],
                                                                   in1=gb[:].rearrange("p (g c) -> p g c", c=16), op0=ALU.mult, op1=ALU.mult),
                 reads=[b_xs2[s], b_ss, b_gb], writes=list(b_actA))
        for g0 in range(0, 128, 8):
            bk = 6 + (g0 // 8) % 2
            for g in range(g0, g0 + 8):
                P.op("pe", lambda e, g=g, g0=g0, bk=bk: e.transpose(out=PSBF[bk][:, (g - g0) * 128:(g - g0 + 1) * 128],
                                                                   in_=u[:, g * 128:(g + 1) * 128], identity=ident_bf[:]),
                     reads=list(b_actA) + [b_const], writes=[b_PS[bk]])
            eng = "act" if (g0 // 8) % 2 == 0 else "dve"
            if eng == "act":
                P.op("act", lambda e, g0=g0, bk=bk: e.activation(out=Up[:, g0:g0 + 8, :].rearrange("p a b -> p (a b)"), in_=PSBF[bk][:], func=AF.Copy),
                     reads=[b_PS[bk]], writes=[b_Up[g0 // 8]])
            else:
                P.op("dve", lambda e, g0=g0, bk=bk: e.tensor_copy(out=Up[:, g0:g0 + 8, :].rearrange("p a b -> p (a b)"), in_=PSBF[bk][:]),
                     reads=[b_PS[bk]], writes=[b_Up[g0 // 8]])
        P.barrier()
        wi_s = [A.at(stg0 + 16384 + 8192 + i * 2048, [128, 8, 128], BF16, "wi%d" % i) for i in range(2)]
        b_wi = [Buf("wi0"), Buf("wi1")]
        for blk in range(16):
            sl = blk % 2
            P.dma("sp", lambda e, sl=sl, blk=blk: e.dma_start(out=wi_s[sl][:], in_=win_d[:, blk * 8:(blk + 1) * 8, :]),
                  reads=[b_wind], writes=[b_wi[sl]])
            bre, bim = next_bank(), next_bank()
            for gl in range(8):
                g = blk * 8 + gl
                gpl, g2 = gl // 2, gl % 2
                for ri, bb in enumerate((bre, bim)):
                    P.op("pe", lambda e, sl=sl, gl=gl, g=g, gpl=gpl, g2=g2, ri=ri, bb=bb: e.matmul(
                        PS[bb][64 * g2:64 * g2 + 64, gpl * 128:(gpl + 1) * 128], lhsT=wi_s[sl][:, gl, ri * 64:(ri + 1) * 64], rhs=Up[:, g, :],
                        start=True, stop=True, tile_position=(0, 64 * g2)),
                        reads=[b_wi[sl], b_Up[g // 8]], writes=[b_PS[bb]])
            P.op("act", lambda e, blk=blk, bre=bre: e.activation(out=Xre[:, blk * 4:(blk + 1) * 4, 1:129], in_=PS[bre][:].rearrange("p (a b) -> p a b", a=4),
                                                                 func=AF.Copy), reads=[b_PS[bre]], writes=[b_X])
            P.op("dve", lambda e, blk=blk, bim=bim: e.tensor_copy(out=Xim[:, blk * 4:(blk + 1) * 4, 1:129], in_=PS[bim][:].rearrange("p (a b) -> p a b", a=4)),
                 reads=[b_PS[bim]], writes=[b_X])
        def level2(src_r, src_i, soff, n, dst_r, dst_i, l, scr):
            h = n // 2
            ar = fap(APW[:], (l * 2 + 0) * 64, [(1, 64), (0, h)])
            ai = fap(APW[:], (l * 2 + 1) * 64, [(1, 64), (0, h)])
            ev = lambda t: fap(t[:, :, :], soff, [(t[:, :, :].ap[1][0], 64), (2, h)])
            od = lambda t: fap(t[:, :, :], soff + 1, [(t[:, :, :].ap[1][0], 64), (2, h)])
            dr, di, sc_ = dst_r[:, :, 0:h], dst_i[:, :, 0:h], scr[:, :, 0:h]
            rd, wr = [b_X, b_tr, b_apw], [b_tr]
            o = lambda fn: P.op("dve", fn, reads=rd, writes=wr)
            o(lambda e: e.tensor_tensor(out=dr, in0=ev(src_r), in1=ar, op=ALU.mult))
            o(lambda e: e.tensor_tensor(out=sc_, in0=ev(src_i), in1=ai, op=ALU.mult))
            o(lambda e: e.tensor_tensor(out=dr, in0=dr, in1=sc_, op=ALU.subtract))
            o(lambda e: e.tensor_tensor(out=dr, in0=dr, in1=od(src_r), op=ALU.add))
            o(lambda e: e.tensor_tensor(out=di, in0=ev(src_r), in1=ai, op=ALU.mult))
            o(lambda e: e.tensor_tensor(out=sc_, in0=ev(src_i), in1=ar, op=ALU.mult))
            o(lambda e: e.tensor_tensor(out=di, in0=di, in1=sc_, op=ALU.add))
            o(lambda e: e.tensor_tensor(out=di, in0=di, in1=od(src_i), op=ALU.add))
        SCR = A.at(wf_off + 32768, [128, 64, 16], F32, "tscr")
        level2(Xre, Xim, 1, 128, TAr, TAi, 0, TBr_big := A.at(stg0, [128, 64, 64], F32, "scrbig"))
        level2(TAr, TAi, 0, 64, TBr, TBi, 1, TCr_big := A.at(stg0 + 16384, [128, 64, 32], F32, "scrbig2"))
        level2(TBr, TBi, 0, 32, TAr, TAi, 2, SCR)
        level2(TAr, TAi, 0, 16, TBr, TBi, 3, SCR)
        level2(TBr, TBi, 0, 8, TAr, TAi, 4, SCR)
        level2(TAr, TAi, 0, 4, TBr, TBi, 5, SCR)
        level2(TBr, TBi, 0, 2, TAr, TAi, 6, SCR)
        P.op("dve", lambda e: e.tensor_copy(out=EE[:, 0, :], in_=TAr[:, :, 0]), reads=[b_tr], writes=[b_EE])
        P.op("dve", lambda e: e.tensor_copy(out=EE[:, 1, :], in_=TAi[:, :, 0]), reads=[b_tr], writes=[b_EE])
        P.dma("pool", lambda e: e.dma_start(out=st_in_d, in_=EE[:].rearrange("p a b -> p (a b)")), reads=[b_EE], writes=[b_stin])
        P.dma("pool", lambda e: e.collective_compute("AllGather", ALU.bypass, replica_groups=[list(range(NCORES))],
                                                     ins=[st_in_d.opt()], outs=[st_out_d.opt()]),
              reads=[b_stin], writes=[b_stout], inc=1)
        P.dma("pool", lambda e: e.dma_start(out=gath[:], in_=st_out_d.rearrange("(r p) c -> p r c", p=128)), reads=[b_stout], writes=[b_sg])
        P.op("dve", lambda e: e.memset(HIN[:], 0.0), writes=[b_hin])
        for j in range(8):
            P.op("dve", lambda e, j=j: e.scalar_tensor_tensor(out=HIN[:], in0=SC[:], scalar=cmask[:, j:j + 1], in1=HIN[:], op0=ALU.mult, op1=ALU.add),
                 reads=[b_sc, b_gvec, b_hin], writes=[b_hin])
            cmul("dve", SC[:, 0, :], SC[:, 1, :], SC[:, 0, :], SC[:, 1, :], A1K[:, 0, :], A1K[:, 1, :], t1[:], t2[:], [b_sc, b_apw], [b_sc])
            P.op("dve", lambda e, j=j: e.tensor_tensor(out=SC[:], in0=SC[:], in1=gath[:, j, :].rearrange("p (a b) -> p a b", a=2), op=ALU.add),
                 reads=[b_sc, b_sg], writes=[b_sc])
        P.op("dve", lambda e: e.tensor_copy(out=Xre[:, :, 0], in_=HIN[:, 0, :]), reads=[b_hin, b_tr], writes=[b_X])
        P.op("dve", lambda e: e.tensor_copy(out=Xim[:, :, 0], in_=HIN[:, 1, :]), reads=[b_hin], writes=[b_X])
        AR_, AI_ = APW[:, 0, 0, :], APW[:, 0, 1, :]
        rdx, wrx = [b_X, b_apw], [b_X]
        o = lambda fn: P.op("dve", fn, reads=rdx, writes=wrx)
        for k in range(128):
            o(lambda e, k=k: e.tensor_tensor(out=t1[:], in0=Xre[:, :, k], in1=AR_, op=ALU.mult))
            o(lambda e, k=k: e.tensor_tensor(out=t2[:], in0=Xim[:, :, k], in1=AI_, op=ALU.mult))
            o(lambda e, k=k: e.tensor_tensor(out=t3[:], in0=Xre[:, :, k], in1=AI_, op=ALU.mult))
            o(lambda e, k=k: e.tensor_tensor(out=t4[:], in0=Xim[:, :, k], in1=AR_, op=ALU.mult))
            o(lambda e, k=k: e.tensor_tensor(out=t1[:], in0=t1[:], in1=t2[:], op=ALU.subtract))
            o(lambda e, k=k: e.tensor_tensor(out=t3[:], in0=t3[:], in1=t4[:], op=ALU.add))
            o(lambda e, k=k: e.tensor_tensor(out=Xre[:, :, k + 1], in0=Xre[:, :, k + 1], in1=t1[:], op=ALU.add))
            o(lambda e, k=k: e.tensor_tensor(out=Xim[:, :, k + 1], in0=Xim[:, :, k + 1], in1=t3[:], op=ALU.add))
        P.barrier()
        so = stg0
        wo_s = [[A.at(so + (i * 2 + j) * 1024, [128, 4, 128], BF16, "wo%d%d" % (i, j)) for j in range(2)] for i in range(2)]; so += 4096
        m0_s = [A.at(so + i * 2048, [128, 8, 128], BF16, "m0s%d" % i) for i in range(2)]; so += 4096
        xb = [[A.at(so + (i * 2 + j) * 256, [128, 128], BF16, "xb%d%d" % (i, j)) for j in range(2)] for i in range(2)]; so += 1024
        Gst = [A.at(so + i * 2048, [128, 8, 128], BF16, "gst%d" % i) for i in range(2)]; so += 4096
        y2 = A.at(so, [128, 512], F32, "y2"); so += 2048
        tg = A.at(so, [128, 512], F32, "tg"); so += 2048
        sg_ = A.at(so, [128, 512], F32, "sg_"); so += 2048
        b_wo = [Buf("wos0"), Buf("wos1")]
        b_m0 = [Buf("m0s0"), Buf("m0s1")]
        b_xb = [Buf("xb0"), Buf("xb1")]
        b_G = [Buf("gst0"), Buf("gst1")]
        b_y2, b_tg, b_sg2 = Buf("y2"), Buf("tg"), Buf("sg2")
        xbc = [0]
        for blk in range(16):
            sl = blk % 2
            P.dma("sp", lambda e, sl=sl, blk=blk: e.dma_start(out=wo_s[sl][0][:], in_=wore_d[:, blk * 4:(blk + 1) * 4, :]),
                  reads=[b_wored], writes=[b_wo[sl]])
            P.dma("sp", lambda e, sl=sl, blk=blk: e.dma_start(out=wo_s[sl][1][:], in_=woim_d[:, blk * 4:(blk + 1) * 4, :]),
                  reads=[b_woimd], writes=[b_wo[sl]])
            P.dma("sp", lambda e, sl=sl, blk=blk: e.dma_start(out=m0_s[sl][:], in_=m0_d[:, blk * 8:(blk + 1) * 8, :]),
                  reads=[b_m0d], writes=[b_m0[sl]])
            bks = [next_bank(), next_bank()]
            for gpl in range(4):
                gp = blk * 4 + gpl
                xi_ = xbc[0] % 2
                xbc[0] += 1
                P.op("act", lambda e, gp=gp, xi_=xi_: e.activation(out=xb[xi_][0][:], in_=Xre[:, gp, 0:128], func=AF.Copy),
                     reads=[b_X], writes=[b_xb[xi_]])
                P.op("pool", lambda e, gp=gp, xi_=xi_: e.tensor_copy(out=xb[xi_][1][:], in_=Xim[:, gp, 0:128]),
                     reads=[b_X], writes=[b_xb[xi_]])
                for g2 in range(2):
                    gl = gpl * 2 + g2
                    g = blk * 8 + gl
                    bk = bks[g2]
                    outp = lambda bk=bk, gpl=gpl: PS[bk][:, gpl * 128:(gpl + 1) * 128]
                    P.op("pe", lambda e, outp=outp, xi_=xi_, g2=g2, sl=sl, gpl=gpl: e.matmul(
                        outp(), lhsT=xb[xi_][0][64 * g2:64 * g2 + 64, :], rhs=wo_s[sl][0][64 * g2:64 * g2 + 64, gpl, :], start=True, stop=False),
                        reads=[b_xb[xi_], b_wo[sl]], writes=[b_PS[bk]])
                    P.op("pe", lambda e, outp=outp, xi_=xi_, g2=g2, sl=sl, gpl=gpl: e.matmul(
                        outp(), lhsT=xb[xi_][1][64 * g2:64 * g2 + 64, :], rhs=wo_s[sl][1][64 * g2:64 * g2 + 64, gpl, :], start=False, stop=False),
                        reads=[b_xb[xi_], b_wo[sl]], writes=[b_PS[bk]])
                    P.op("pe", lambda e, outp=outp, g=g, sl=sl, gl=gl: e.matmul(
                        outp(), lhsT=Up[:, g, :], rhs=m0_s[sl][:, gl, :], start=False, stop=True),
                        reads=[b_Up[g // 8], b_m0[sl]], writes=[b_PS[bk]])
            for g2 in range(2):
                bk = bks[g2]
                P.op("act", lambda e, bk=bk: e.activation(out=y2[:], in_=PS[bk][:], func=AF.Square), reads=[b_PS[bk]], writes=[b_y2])
                P.op("dve", lambda e: e.tensor_scalar(out=tg[:], in0=y2[:], scalar1=0.044715, scalar2=1.0, op0=ALU.mult, op1=ALU.add),
                     reads=[b_y2], writes=[b_tg])
                P.op("dve", lambda e, bk=bk: e.tensor_tensor(out=tg[:], in0=tg[:], in1=PS[bk][:], op=ALU.mult), reads=[b_tg, b_PS[bk]], writes=[b_tg])
                P.op("act", lambda e: e.activation(out=sg_[:], in_=tg[:], func=AF.Sigmoid, scale=1.5957691216057308), reads=[b_tg], writes=[b_sg2])
                P.op("dve", lambda e, bk=bk, sl=sl, g2=g2: e.tensor_tensor(
                    out=fap(Gst[sl][:], g2 * 16, [(32, 4), (128, 8), (1, 16)]),
                    in0=sg_[:].rearrange("p (a b c) -> p a b c", a=4, b=8), in1=PS[bk][:].rearrange("p (a b c) -> p a b c", a=4, b=8), op=ALU.mult),
                    reads=[b_sg2, b_PS[bk]], writes=[b_G[sl]])
            for r in range(8):
                P.op("pe", lambda e, r=r, sl=sl: e.transpose(out=PSBF[7][:, r * 128:(r + 1) * 128], in_=Gst[sl][:, r, :], identity=ident_bf[:]),
                     reads=[b_G[sl], b_const], writes=[b_PS[7]])
            P.op("act", lambda e, blk=blk: e.activation(out=fap(actA[:, blk, :], 0, [(1, 8), (8, 128)]), in_=PSBF[7][:].rearrange("p (a b) -> p a b", a=8),
                                                        func=AF.Copy), reads=[b_PS[7]], writes=[b_actA[blk]])


    def glu():
        P.barrier()
        tmp = A.at(phase_off, [128, 2, 512], F32, "glut")
        b_tmp = [Buf("glut0"), Buf("glut1")]
        rhsA = lambda k, tb: actA[:, k, tb * 512:(tb + 1) * 512]
        for c in range(KC):
            iv_ = load_w(w_glu, KC, c * 128)
            ig_ = load_w(w_glu, KC, 2048 + c * 128)
            for tb in range(2):
                bv_, bg_ = next_bank(), next_bank()
                for (iw, bb) in ((iv_, bv_), (ig_, bg_)):
                    for k in range(KC):
                        P.op("pe", lambda e, iw=iw, bb=bb, k=k, tb=tb: e.matmul(
                            PS[bb][:], lhsT=WB[iw][:, k * 128:(k + 1) * 128], rhs=rhsA(k, tb), start=(k == 0), stop=(k == KC - 1)),
                            reads=[b_WB[iw], b_actA[k]], writes=[b_PS[bb]])
                P.op("act", lambda e, tb=tb, bg_=bg_: e.activation(out=tmp[:, tb, :], in_=PS[bg_][:], func=AF.Sigmoid),
                     reads=[b_PS[bg_]], writes=[b_tmp[tb]])
                P.op("dve", lambda e, tb=tb, bv_=bv_: e.tensor_tensor(out=tmp[:, tb, :], in0=tmp[:, tb, :], in1=PS[bv_][:], op=ALU.mult),
                     reads=[b_tmp[tb], b_PS[bv_]], writes=[b_tmp[tb]])
                P.op("dve", lambda e, tb=tb, c=c: e.tensor_tensor(out=hT[:, c, tb * 512:(tb + 1) * 512], in0=hT[:, c, tb * 512:(tb + 1) * 512],
                                                                in1=tmp[:, tb, :], op=ALU.add),
                     reads=[b_tmp[tb], b_hT[c]], writes=[b_hT[c]])

    def final_out(ps_):
        P.barrier()
        yT, b_yT = hT, b_hT
        rmsnorm_T(4, yT, b_yT)
        ot = [A.at(phase_off + 4096 + i * D * 4, [128, D], F32, "ot%d" % i) for i in range(2)]
        b_ot = [Buf("ot0"), Buf("ot1")]
        for tt in range(NT // 128):
            s = tt % 2
            for c4 in range(KC // 4):
                bk = next_bank()
                for j in range(4):
                    c = c4 * 4 + j
                    P.op("pe", lambda e, bk=bk, j=j, c=c, tt=tt: e.transpose(
                        out=PS[bk][:, j * 128:(j + 1) * 128], in_=yT[:, c, tt * 128:(tt + 1) * 128], identity=ident_f[:]),
                        reads=[b_yT[c], b_const], writes=[b_PS[bk]])
                if c4 % 2 == 0:
                    P.op("act", lambda e, bk=bk, c4=c4, s=s: e.activation(out=ot[s][:, c4 * 512:(c4 + 1) * 512], in_=PS[bk][:], func=AF.Copy),
                         reads=[b_PS[bk]], writes=[b_ot[s]])
                else:
                    P.op("dve", lambda e, bk=bk, c4=c4, s=s: e.tensor_copy(out=ot[s][:, c4 * 512:(c4 + 1) * 512], in_=PS[bk][:]),
                         reads=[b_PS[bk]], writes=[b_ot[s]])
            P.dma("sp", lambda e, s=s, tt=tt: e.dma_start(out=out_d[ps_, tt * 128:(tt + 1) * 128, :], in_=ot[s][:]),
                  reads=[b_ot[s]], writes=[b_out], sync=b_ot[s])
        return b_ot

    phase_off = off
    tails = []
    if not SKIP_S5:
        s5_prologue()
    for ps_ in range(NPASS):
        if not SKIP_S5:
            s5_pass(ps_)
        load_x_to_hT(ps_)
        if not SKIP_S5:
            glu()
        if STAGE >= "C" and not LITE:
            ffn(0, 1)
        if STAGE >= "D":
            attention(ps_)
        if STAGE >= "E" and not LITE:
            ffn(1, 3)
        tails += final_out(ps_)
    P.wait_all("sp", tails + [b_out])
    P.emit(nc)
    return nc


_CACHE = {}


def kernel(**inputs):
    x = np.ascontiguousarray(inputs["x"], dtype=np.float32)
    if "nc" not in _CACHE:
        _CACHE["nc"] = build_program()
    nc = _CACHE["nc"]
    in_maps = []
    for c in range(NCORES):
        m = {}
        xs = np.stack([x[0, (8 * p + c) * NT:(8 * p + c + 1) * NT, :] for p in range(NPASS)], 0)
        m["x"] = np.ascontiguousarray(xs)
        m["norm_mix"] = inputs["norm_mix"]
        m["s5_a_re"] = inputs["s5_a_re"][0]
        m["s5_a_im"] = inputs["s5_a_im"][0]
        m["s5_log_step"] = inputs["s5_log_step"][0].reshape(128, 1)
        m["s5_b_re"] = inputs["s5_b_re"][0].reshape(128, 1024)
        m["s5_b_im"] = inputs["s5_b_im"][0].reshape(128, 1024)
        m["s5_c_re"] = inputs["s5_c_re"][0].reshape(128, 1024)
        m["s5_c_im"] = inputs["s5_c_im"][0].reshape(128, 1024)
        m["s5_d"] = inputs["s5_d"][0]
        m["s5_w_glu"] = inputs["s5_w_glu"][0]
        m["attn_w_qkv"] = inputs["attn_w_qkv"][0]
        m["attn_b_qkv"] = inputs["attn_b_qkv"][0]
        m["attn_sinks"] = inputs["attn_sinks"][0]
        m["attn_w_o"] = inputs["attn_w_o"][0]
        m["norm_ffn"] = inputs["norm_ffn"]
        m["ffn_w_gate"] = inputs["ffn_w_gate"]
        m["ffn_w_up"] = inputs["ffn_w_up"]
        m["ffn_w_down"] = inputs["ffn_w_down"]
        m["norm_final"] = inputs["norm_final"]
        cm = np.zeros((128, 16), np.float32)
        cm[:, c] = 1.0
        m["cmask"] = cm
        hm = np.zeros((128, NPASS), np.float32)
        if c == 0:
            hm[:, 0] = 1.0e9
        m["hmask"] = hm
        if LITE:
            for k in ("s5_w_glu",):
                m[k] = np.zeros((128, 128), np.float32)
            for k in ("ffn_w_gate", "ffn_w_up", "ffn_w_down"):
                m[k] = np.zeros((2, 128, 128), np.float32)
        in_maps.append({k: np.ascontiguousarray(v, dtype=np.float32) for k, v in m.items()})
    res = run_bass_kernel_spmd(nc, in_maps, core_ids=list(range(NCORES)))
    out = np.empty((1, NCORES * NPASS * NT, D), np.float32)
    for c in range(NCORES):
        o = res.results[c]["out"]
        for p in range(NPASS):
            out[0, (8 * p + c) * NT:(8 * p + c + 1) * NT, :] = o[p]
    return out
```
